# Optimizing a Trainium2 kernel written in Bass

```python
import math
import jax, jax.numpy as jnp
from jax import lax
import numpy as np

D_MODEL = 4096
BATCH = 8
SEQ = 2048
DEPTH = 1
DEC_BATCH = 1
DEC_SEQ = 16384
PAST_LEN = 128

MIX_WIDTH = D_MODEL
ATTN_WIDTH = MIX_WIDTH // 2
HGRN_WIDTH = MIX_WIDTH - ATTN_WIDTH
DIFF_HEAD_DIM = 128
N_DIFF_HEADS = ATTN_WIDTH // (2 * DIFF_HEAD_DIM)
HGRN_HEAD_DIM = 128
N_HGRN_HEADS = HGRN_WIDTH // HGRN_HEAD_DIM
D_FF = 4 * D_MODEL
ROPE_THETA = 10000.0
NORM_EPS = 1e-6
Q_BLOCK = 128
CHUNK = 64
IN_PROJ_WIDTH = 3 * ATTN_WIDTH + 5 * HGRN_WIDTH

kernel_name = 'hymba_diffattn_hgrn2_encoder'


def rms_norm(x, w):
    xf = x.astype(jnp.float32)
    y = xf * lax.rsqrt(jnp.mean(xf * xf, axis=-1, keepdims=True) + NORM_EPS)
    return (y * w.astype(jnp.float32)).astype(x.dtype)


def rope_tables(length):
    inv = 1.0 / (ROPE_THETA ** (jnp.arange(0, DIFF_HEAD_DIM, 2, dtype=jnp.float32) / DIFF_HEAD_DIM))
    ang = jnp.arange(length, dtype=jnp.float32)[:, None] * inv[None, :]
    ang = jnp.concatenate([ang, ang], axis=-1)
    return jnp.cos(ang), jnp.sin(ang)


def apply_rope(x, cos, sin):
    half = DIFF_HEAD_DIM // 2
    x1, x2 = x[..., :half], x[..., half:]
    rot = jnp.concatenate([-x2, x1], axis=-1)
    c = cos[None, :, None, None, :]
    s = sin[None, :, None, None, :]
    return (x.astype(jnp.float32) * c + rot.astype(jnp.float32) * s).astype(x.dtype)


def diff_attention(q_a, k_a, v_a, lam, cos, sin):
    B, L, _ = q_a.shape
    q = apply_rope(q_a.reshape(B, L, N_DIFF_HEADS, 2, DIFF_HEAD_DIM), cos, sin)
    k = apply_rope(k_a.reshape(B, L, N_DIFF_HEADS, 2, DIFF_HEAD_DIM), cos, sin)
    v = v_a.reshape(B, L, N_DIFF_HEADS, 2 * DIFF_HEAD_DIM)
    n_blocks = L // Q_BLOCK
    q_blocks = q.reshape(B, n_blocks, Q_BLOCK, N_DIFF_HEADS, 2, DIFF_HEAD_DIM).transpose(1, 0, 2, 3, 4, 5)
    scale = DIFF_HEAD_DIM ** -0.5

    def attend(qb):
        s = jnp.einsum('bqhcd,bkhcd->bhcqk', qb, k).astype(jnp.float32) * scale
        p = jax.nn.softmax(s, axis=-1)
        p_diff = p[:, :, 0] - lam * p[:, :, 1]
        return jnp.einsum('bhqk,bkhe->bqhe', p_diff.astype(v.dtype), v)

    o = lax.map(attend, q_blocks)
    return o.transpose(1, 0, 2, 3, 4).reshape(B, L, N_DIFF_HEADS, 2 * DIFF_HEAD_DIM)


def gla_chunkwise(q, k, v, log_f):
    B, L, H, DK = q.shape
    DV = v.shape[-1]
    n_chunks = L // CHUNK

    def to_chunks(t):
        return t.reshape(B, n_chunks, CHUNK, H, t.shape[-1]).transpose(1, 0, 3, 2, 4)

    mask = jnp.tril(jnp.ones((CHUNK, CHUNK), dtype=bool))[None, None, :, :, None]

    def step(S, inp):
        qb, kb, vb, gb = inp
        b = jnp.cumsum(gb, axis=2)
        rel = jnp.where(mask, b[:, :, :, None, :] - b[:, :, None, :, :], -jnp.inf)
        A = jnp.einsum('bhtk,bhsk,bhtsk->bhts', qb, kb, jnp.exp(rel))
        o = jnp.einsum('bhts,bhsv->bhtv', A, vb) + jnp.einsum('bhtk,bhkv->bhtv', qb * jnp.exp(b), S)
        b_last = b[:, :, -1:, :]
        S_new = jnp.exp(b_last[:, :, 0, :])[..., None] * S + jnp.einsum(
            'bhsk,bhsv->bhkv', kb * jnp.exp(b_last - b), vb)
        return S_new, o

    S0 = jnp.zeros((B, H, DK, DV), dtype=jnp.float32)
    _, o = lax.scan(step, S0, (to_chunks(q), to_chunks(k), to_chunks(v), to_chunks(log_f)))
    return o.transpose(1, 0, 3, 2, 4).reshape(B, L, H, DV)


def hgrn2_bidirectional(q_h, f_fw, f_bw, i_h, g_h, lb_fw, lb_bw, norm_w):
    B, L, _ = q_h.shape
    shp = (B, L, N_HGRN_HEADS, HGRN_HEAD_DIM)
    q = jax.nn.silu(q_h.astype(jnp.float32)).reshape(shp)
    v = i_h.astype(jnp.float32).reshape(shp)

    def gates(f_raw, lb):
        f = lb + (1.0 - lb) * jax.nn.sigmoid(f_raw.astype(jnp.float32))
        return (1.0 - f).reshape(shp), jnp.log(f).reshape(shp)

    k_fw, lf_fw = gates(f_fw, lb_fw)
    k_bw, lf_bw = gates(f_bw, lb_bw)
    o_fw = gla_chunkwise(q, k_fw, v, lf_fw)
    flip = lambda t: jnp.flip(t, axis=1)
    o_bw = flip(gla_chunkwise(flip(q), flip(k_bw), flip(v), flip(lf_bw)))
    o = rms_norm(o_fw + o_bw, norm_w) * jax.nn.silu(g_h.astype(jnp.float32)).reshape(shp)
    return o.reshape(B, L, HGRN_WIDTH).astype(q_h.dtype)


def encoder_trunk(x, attn_norm_w, w_in, diff_lambda, subln_w, hgrn_lb, hgrn_norm_w,
                  w_out, mlp_norm_w, w_up, w_down, final_norm_w):
    B, L, _ = x.shape
    cos, sin = rope_tables(L)
    lb_all = jnp.cumsum(jax.nn.softmax(hgrn_lb.astype(jnp.float32), axis=1), axis=1)
    bounds = [ATTN_WIDTH, 2 * ATTN_WIDTH, 3 * ATTN_WIDTH] + [3 * ATTN_WIDTH + j * HGRN_WIDTH for j in range(1, 5)]
    for layer in range(DEPTH):
        h = rms_norm(x, attn_norm_w[layer])
        proj = h @ w_in[layer]
        q_a, k_a, v_a, q_h, f_fw, f_bw, i_h, g_h = jnp.split(proj, bounds, axis=-1)
        lam_init = 0.8 - 0.6 * math.exp(-0.3 * layer)
        lq1, lk1, lq2, lk2 = diff_lambda[layer].astype(jnp.float32)
        lam = jnp.exp(jnp.sum(lq1 * lk1)) - jnp.exp(jnp.sum(lq2 * lk2)) + lam_init
        o_a = diff_attention(q_a, k_a, v_a, lam, cos, sin)
        o_a = (rms_norm(o_a, subln_w[layer]) * (1.0 - lam_init)).reshape(B, L, ATTN_WIDTH)
        o_h = hgrn2_bidirectional(q_h, f_fw, f_bw, i_h, g_h, lb_all[0, layer], lb_all[1, layer], hgrn_norm_w[layer])
        x = x + jnp.concatenate([o_a, o_h], axis=-1) @ w_out[layer]
        h = rms_norm(x, mlp_norm_w[layer])
        x = x + jnp.square(jax.nn.relu(h @ w_up[layer])) @ w_down[layer]
    return rms_norm(x, final_norm_w)


def setup_inputs(seed: int = 0) -> dict:
    key = jax.random.key(seed)
    ks = jax.random.split(key, 13)
    f32 = jnp.float32

    def normal(k, shape, scale):
        return jax.random.normal(k, shape, f32) * scale

    return {
        'x_prompt': normal(ks[0], (BATCH, SEQ, D_MODEL), 1.0),
        'x_sample': normal(ks[1], (DEC_BATCH, DEC_SEQ, D_MODEL), 1.0),
        'attn_norm_w': 1.0 + normal(ks[2], (DEPTH, D_MODEL), 0.01),
        'w_in': normal(ks[3], (DEPTH, D_MODEL, IN_PROJ_WIDTH), D_MODEL ** -0.5),
        'diff_lambda': normal(ks[4], (DEPTH, 4, DIFF_HEAD_DIM), 0.1),
        'subln_w': 1.0 + normal(ks[5], (DEPTH, 2 * DIFF_HEAD_DIM), 0.01),
        'hgrn_lb': normal(ks[6], (2, DEPTH + 1, HGRN_WIDTH), 0.1),
        'hgrn_norm_w': 1.0 + normal(ks[7], (DEPTH, HGRN_HEAD_DIM), 0.01),
        'w_out': normal(ks[8], (DEPTH, MIX_WIDTH, D_MODEL), MIX_WIDTH ** -0.5),
        'mlp_norm_w': 1.0 + normal(ks[9], (DEPTH, D_MODEL), 0.01),
        'w_up': normal(ks[10], (DEPTH, D_MODEL, D_FF), D_MODEL ** -0.5),
        'w_down': normal(ks[11], (DEPTH, D_FF, D_MODEL), D_FF ** -0.5),
        'final_norm_w': 1.0 + normal(ks[12], (D_MODEL,), 0.01),
    }


def reference(x_prompt, x_sample, attn_norm_w, w_in, diff_lambda, subln_w, hgrn_lb, hgrn_norm_w,
              w_out, mlp_norm_w, w_up, w_down, final_norm_w):
    y_prompt = encoder_trunk(x_prompt, attn_norm_w, w_in, diff_lambda, subln_w, hgrn_lb, hgrn_norm_w,
                             w_out, mlp_norm_w, w_up, w_down, final_norm_w)
    y_sample = encoder_trunk(x_sample, attn_norm_w, w_in, diff_lambda, subln_w, hgrn_lb, hgrn_norm_w,
                             w_out, mlp_norm_w, w_up, w_down, final_norm_w)
    return (y_prompt, y_sample)
```

```python
import math
import numpy as np
import concourse.bass as bass
import concourse.mybir as mybir
from concourse.bass_utils import run_bass_kernel_spmd

F32 = mybir.dt.float32
BF16 = mybir.dt.bfloat16
AF = mybir.ActivationFunctionType
ALU = mybir.AluOpType
AX = mybir.AxisListType
EPS = 1e-6
ENG = ['pe', 'act', 'dve', 'pool', 'sp']


class Buf:
    __slots__ = ('name', 'w', 'r')

    def __init__(self, name):
        self.name = name
        self.w = None
        self.r = {}


class Op:
    __slots__ = ('eng', 'fn', 'seq', 'dma', 'key', 'cnt', 'signal', 'waits', 'inc', 'sigval', 'dyn', 'region')


class Sched:
    def __init__(self):
        self.ops = {e: [] for e in ENG}
        self.known = {e: {} for e in ENG}
        self.dmacnt = {}
        self.region = None
        self.pending = {}

    def barrier(self):
        last = {e: (self.ops[e][-1] if self.ops[e] else None) for e in ENG}
        dm = dict(self.dmacnt)
        lastdma = {}
        for e in ENG:
            for o in self.ops[e]:
                if o.dma:
                    lastdma[o.key] = o
        for e in ENG:
            self.pending[e] = (last, lastdma)

    def op(self, eng, fn, reads=(), writes=(), key=None, inc=16, dyn=False):
        o = Op()
        o.dyn = dyn
        o.region = self.region
        o.eng = eng; o.fn = fn; o.seq = len(self.ops[eng]); o.dma = key is not None
        o.key = key; o.signal = False; o.waits = []; o.inc = inc; o.cnt = 0; o.sigval = 0
        if o.dma:
            self.dmacnt[key] = self.dmacnt.get(key, 0) + inc
            o.cnt = self.dmacnt[key]
        deps = []
        for b in reads:
            if b.w is not None:
                deps.append((b.w, 0))
        for b in writes:
            if b.w is not None:
                deps.append((b.w, 1))
            for p in b.r.values():
                deps.append((p, 2))
        kn = self.known[eng]
        if eng in self.pending:
            last, lastdma = self.pending.pop(eng)
            for e2 in ENG:
                p = last[e2]
                if p is None or p.dma or e2 == eng:
                    continue
                if kn.get(e2, -1) >= p.seq:
                    continue
                kn[e2] = p.seq
                p.signal = True
                o.waits.append(p)
            for k, p in lastdma.items():
                if k.startswith('ag') and eng not in ('pool', 'sp'):
                    continue
                if kn.get('d:' + k, 0) >= p.cnt:
                    continue
                kn['d:' + k] = p.cnt
                o.waits.append(p)

        for p, kind in deps:
            if p is o:
                continue
            if p.dma:
                if o.dma and p.key == o.key and kind == 1:
                    continue
                k = 'd:' + p.key
                if kn.get(k, 0) >= p.cnt:
                    continue
                kn[k] = p.cnt
                o.waits.append(p)
            else:
                if p.eng == eng and not o.dma:
                    if kind == 0 and eng != 'pe' and o.seq - p.seq <= 2:
                        p.signal = True
                        o.waits.append(p)
                    continue
                if kn.get(p.eng, -1) >= p.seq:
                    continue
                kn[p.eng] = p.seq
                p.signal = True
                o.waits.append(p)
        rk = ('d:' + key) if o.dma else eng
        for b in reads:
            b.r[rk] = o
        for b in writes:
            b.w = o
            b.r = {}
        self.ops[eng].append(o)
        return o

    def simulate(self):
        for e in ENG:
            c = 0
            for o in self.ops[e]:
                if (not o.dma) and o.signal:
                    c += 1
                    o.sigval = c
        sem = {}
        ptr = {e: 0 for e in ENG}
        prog = True
        while prog:
            prog = False
            for e in ENG:
                while ptr[e] < len(self.ops[e]):
                    o = self.ops[e][ptr[e]]
                    ok = True
                    for p in o.waits:
                        if p.dma:
                            if sem.get('d:' + p.key, 0) < p.cnt:
                                ok = False
                        else:
                            if sem.get(p.eng, 0) < p.sigval:
                                ok = False
                    if not ok:
                        break
                    if o.dma:
                        sem['d:' + o.key] = sem.get('d:' + o.key, 0) + o.inc
                    elif o.signal:
                        sem[e] = sem.get(e, 0) + 1
                    ptr[e] += 1
                    prog = True
        for e in ENG:
            if ptr[e] < len(self.ops[e]):
                o = self.ops[e][ptr[e]]
                print("DEADLOCK", e, ptr[e], len(self.ops[e]), [((p.eng, p.key, p.cnt, sem.get('d:' + p.key, 0)) if p.dma else (p.eng, p.seq, p.sigval, sem.get(p.eng, 0))) for p in o.waits], flush=True)
        print("simulate done", {e: (ptr[e], len(self.ops[e])) for e in ENG}, flush=True)

    def emit(self, nc, final_keys, pid=None):
        sems = {}
        from contextlib import ExitStack
        with ExitStack() as st:
            for e in ENG:
                sems[e] = st.enter_context(nc.semaphore('s_' + e))
            for k in self.dmacnt:
                sems['d:' + k] = st.enter_context(nc.semaphore('d_' + k))
            for e in ENG:
                c = 0
                for o in self.ops[e]:
                    if (not o.dma) and o.signal:
                        c += 1
                        o.sigval = c
            print('free sems after alloc', nc.free_len(), 'nkeys', len(self.dmacnt), flush=True)
            block = st.enter_context(nc.Block())

            def emit_one(e, eng, o, core):
                for p in o.waits:
                    if p.dma:
                        eng.wait_ge(sems['d:' + p.key], p.cnt)
                    else:
                        eng.wait_ge(sems[p.eng], p.sigval)
                ins = o.fn(eng, core) if o.dyn else o.fn(eng)
                if o.dma:
                    ins.then_inc(sems['d:' + o.key], o.inc)
                elif o.signal:
                    ins.then_inc(sems[e], 1)

            def run(e, eng):
                ops = self.ops[e]
                i = 0
                pidv = nc.partition_id([mybir.EngineType.SP]) if e == 'sp' else None
                while i < len(ops):
                    if e != 'sp' or ops[i].region is None:
                        emit_one(e, eng, ops[i], None)
                        i += 1
                        continue
                    j = i
                    while j < len(ops) and ops[j].region == ops[i].region and j - i < 1200:
                        j += 1
                    for core in nc.Switch(engines=[eng], index=[pidv], n=8):
                        for o in ops[i:j]:
                            emit_one(e, eng, o, core)
                    i = j
                if e == 'sp':
                    for k in final_keys:
                        eng.wait_ge(sems['d:' + k], self.dmacnt[k])

            @block.tensor
            def _(eng):
                run('pe', eng)

            @block.scalar
            def _(eng):
                run('act', eng)

            @block.vector
            def _(eng):
                run('dve', eng)

            @block.gpsimd
            def _(eng):
                run('pool', eng)

            @block.sync
            def _(eng):
                run('sp', eng)


def bmid(ap2, reps):
    a = ap2.ap
    return bass.AP(ap2.tensor, ap2.offset, [list(a[0]), [0, reps], list(a[1])])


def blast(ap2, reps):
    a = ap2.ap
    return bass.AP(ap2.tensor, ap2.offset, [list(a[0]), list(a[1]), [0, reps]])


def build(SEG, NF, debug=False):
    D = 4096
    NIN = 16384
    KT = D // 128
    LS = 8 * SEG
    NTILE = SEG // 512
    assert SEG % 512 == 0 and NF % 4096 == 0
    nc = bass.Bass("TRN2", target_bir_lowering=False)
    S = Sched()
    SP = mybir.EngineType.SP
    pid = None

    def din(name, shape, dt=F32):
        return nc.dram_tensor(name, shape, dt, kind="ExternalInput").ap()

    xin = {'p': din("xp", [SEG, D]), 's': din("xs", [SEG, D])}
    w_sh = {'in': din("w_in_sh", [D // 8, NIN]), 'out': din("w_out_sh", [D // 8, D]),
            'up': din("w_up_sh", [D // 8, NF]), 'down': din("w_down_sh", [NF // 8, D])}
    ancol_d = din("ancol", [128, KT]); mncol_d = din("mncol", [128, KT])
    finw_d = din("finw", [1, D]); dlam_d = din("dlam", [1, 512]); subcol_d = din("subcol", [128, 2])
    lbt_d = din("lbt", [128, 64]); hnw_d = din("hnw", [1, 128])
    cos_d = {'p': din("cosp", [SEG, 128]), 's': din("coss", [SEG, 128])}
    sin_d = {'p': din("sinp", [SEG, 128]), 's': din("sins", [SEG, 128])}
    ident_d = din("ident", [128, 128]); mask_d = din("masks", [64, 128]); scanm_d = din("scanm", [128, 512])
    yout = {'p': nc.dram_tensor("yp", [SEG, D], F32, kind="ExternalOutput").ap(),
            's': nc.dram_tensor("ys", [SEG, D], F32, kind="ExternalOutput").ap()}

    DBG = ("p_AQK", "p_AV", "p_HQK", "p_HV", "p_HG", "p_HD", "OT_P", "OBW_P")

    def dint(name, shape, dt=BF16):
        if debug and name in DBG:
            return nc.dram_tensor(name, shape, dt, kind="ExternalOutput").ap()
        return nc.dram_tensor(name, shape, dt).ap()

    wshape = {'in': (D, NIN), 'out': (D, D), 'up': (D, NF), 'down': (NF, D)}
    wsrc = {k: dint("wsrc_" + k, [wshape[k][0] // 8, wshape[k][1]]) for k in wshape}
    wfull = {k: dint("wfull_" + k, list(wshape[k])) for k in wshape}
    wbuf = {k: Buf("wfull_" + k) for k in wshape}
    scr = {}
    sbufs = {}
    for sg in ('p', 's'):
        scr[sg] = {'AQK': dint(sg + "_AQK", [4096, SEG]), 'AV': dint(sg + "_AV", [SEG, 2048]),
                   'HQK': dint(sg + "_HQK", [8192, SEG]), 'HV': dint(sg + "_HV", [SEG, 2048]),
                   'HG': dint(sg + "_HG", [SEG, 2048]), 'HD': dint(sg + "_HD", [4096, SEG // 64], F32)}
        sbufs[sg] = {k: Buf(sg + k) for k in scr[sg]}
    gat = {'AQK': dint("g_AQK", [8 * 4096, SEG]), 'AV': dint("g_AV", [LS, 2048]),
           'HQK': dint("g_HQK", [8 * 8192, SEG]), 'HV': dint("g_HV", [LS, 2048]),
           'HG': dint("g_HG", [LS, 2048]), 'HD': dint("g_HD", [8 * 4096, SEG // 64], F32)}
    gbufs = {k: Buf("g" + k) for k in gat}
    OT_P = dint("OT_P", [4096, SEG]); b_OT_P = Buf("OT_P")
    OTa_s = dint("OTa_s", [256, LS]); OTh_s = dint("OTh_s", [256, LS])
    b_OTa_s = Buf("OTa_s"); b_OTh_s = Buf("OTh_s")
    G_OTa = dint("G_OTa", [2048, LS]); G_OTh = dint("G_OTh", [2048, LS])
    b_GOTa = Buf("G_OTa"); b_GOTh = Buf("G_OTh")
    OBW_P = dint("OBW_P", [16 * SEG, 128], F32); OBW_S = dint("OBW_S", [2 * LS, 128], F32)
    b_OBW = Buf("OBW")

    class Arena:
        def __init__(self):
            self.off = 16640
            self.n = 0

        def alloc(self, shape, dt):
            nb = 2 if dt == BF16 else 4
            size = int(np.prod(shape[1:])) * nb
            size = (size + 63) // 64 * 64
            self.n += 1
            h = nc.alloc_sbuf_tensor_at("sb%d" % self.n, list(shape), dt, offset=self.off)
            self.off += size
            assert self.off <= 229376, self.off
            return h.ap()

    def alloc_at(off, shape, dt):
        ar.n += 1
        return nc.alloc_sbuf_tensor_at("sb%d" % ar.n, list(shape), dt, offset=off).ap()

    ar = Arena()
    ident_f = ar.alloc([128, 128], F32); ident_b = ar.alloc([128, 128], BF16)
    ones_b = ar.alloc([128, 128], BF16); ones_f = ar.alloc([128, 128], F32)
    masks = ar.alloc([64, 128], F32); scanm = ar.alloc([128, 512], F32)
    ancol = ar.alloc([128, KT], F32); mncol = ar.alloc([128, KT], F32)
    subcol = ar.alloc([128, 2], F32); lbt = ar.alloc([128, 64], F32); lbe = ar.alloc([128, 64], F32)
    lb = ar.alloc([128, 32], F32); oml = ar.alloc([128, 32], F32); lden = ar.alloc([128, 32], F32)
    hnw = ar.alloc([64, 128], F32); dl_bc = ar.alloc([128, 512], F32); lprod = ar.alloc([128, 256], F32)
    lsum = ar.alloc([128, 2], F32); lexp = ar.alloc([128, 2], F32); nlam = ar.alloc([128, 1], F32)
    cbuf = Buf("consts")
    base_off = ar.off
    psf = [nc.alloc_psum_tensor("psf%d" % i, [128, 512], F32).ap() for i in range(8)]
    psb = psf[7].bitcast(BF16)
    pbuf = [Buf("psf%d" % i) for i in range(8)]
    pbb = pbuf[7]

    def ld(eng, out, in_, rb, wb, key):
        def f(e):
            try:
                return e.dma_start(out=out, in_=in_)
            except Exception:
                print("DMA FAIL", key, out, in_, flush=True)
                raise
        S.op(eng, f, rb, wb, key=key)

    dbg_n = {'i': 0}

    def dump(name, ap, rb, shape, dt=F32):
        if not debug:
            return
        t = nc.dram_tensor("dbg_" + name, list(shape), dt, kind="ExternalOutput").ap()
        dbg_n['i'] += 1
        S.op('pool', lambda e: e.dma_start(out=t, in_=ap), rb, [], key='dbg%d' % dbg_n['i'])

    def ldd(out, in_fn, rb, wb, key):
        S.op('sp', lambda e, core: e.dma_start(out=out, in_=in_fn(core)), rb, wb, key=key, dyn=True)

    for (o_, i_) in ((ident_f, ident_d), (masks, mask_d), (scanm, scanm_d), (ancol, ancol_d), (mncol, mncol_d),
                     (subcol, subcol_d), (lbt, lbt_d)):
        ld('sp', o_, i_, [], [cbuf], 'const')
    ld('sp', hnw, hnw_d.partition_broadcast(64), [], [cbuf], 'const')
    ld('sp', dl_bc, dlam_d.partition_broadcast(128), [], [cbuf], 'const')
    S.op('dve', lambda e: e.tensor_copy(out=ident_b, in_=ident_f), [cbuf], [cbuf])
    S.op('dve', lambda e: e.memset(ones_b, 1.0), [], [cbuf])
    S.op('dve', lambda e: e.memset(ones_f, 1.0), [], [cbuf])
    S.op('dve', lambda e: e.tensor_scalar(subcol, subcol, 0.8, None, op0=ALU.mult), [cbuf], [cbuf])
    dl3 = dl_bc.rearrange("p (a b) -> p a b", a=4)
    lp3 = lprod.rearrange("p (a b) -> p a b", a=2)
    S.op('dve', lambda e: e.tensor_tensor(out=lp3, in0=bass.AP(dl_bc.tensor, dl_bc.offset, [list(dl_bc.ap[0]), [256, 2], [1, 128]]),
                                          in1=bass.AP(dl_bc.tensor, dl_bc.offset + 128, [list(dl_bc.ap[0]), [256, 2], [1, 128]]),
                                          op=ALU.mult), [cbuf], [cbuf])
    S.op('dve', lambda e: e.tensor_reduce(out=lsum, in_=lp3, axis=AX.X, op=ALU.add), [cbuf], [cbuf])
    S.op('act', lambda e: e.activation(out=lexp, in_=lsum, func=AF.Exp), [cbuf], [cbuf])
    S.op('dve', lambda e: e.tensor_tensor(out=nlam, in0=lexp[:, 1:2], in1=lexp[:, 0:1], op=ALU.subtract), [cbuf], [cbuf])
    S.op('dve', lambda e: e.tensor_scalar(nlam, nlam, -0.2, None, op0=ALU.add), [cbuf], [cbuf])
    S.op('act', lambda e: e.activation(out=lbe, in_=lbt, func=AF.Exp), [cbuf], [cbuf])
    le4 = lbe.rearrange("p (d l j) -> p d l j", d=2, l=2)
    lb3 = lb.rearrange("p (d j) -> p d j", d=2)
    ld3 = lden.rearrange("p (d j) -> p d j", d=2)
    S.op('dve', lambda e: e.tensor_tensor(out=ld3, in0=le4[:, :, 0, :], in1=le4[:, :, 1, :], op=ALU.add), [cbuf], [cbuf])
    S.op('dve', lambda e: e.reciprocal(out=lden, in_=lden), [cbuf], [cbuf])
    S.op('dve', lambda e: e.tensor_tensor(out=lb3, in0=le4[:, :, 0, :], in1=ld3, op=ALU.mult), [cbuf], [cbuf])
    S.op('dve', lambda e: e.tensor_scalar(oml, lb, -1.0, 1.0, op0=ALU.mult, op1=ALU.add), [cbuf], [cbuf])

    RG = [list(range(8))]
    for k in ('in', 'out', 'up', 'down'):
        rows = wshape[k][0] // 8
        nsp = 4
        rs = rows // nsp
        wsb = Buf("wsrc" + k)
        for i in range(nsp):
            S.op('pool', lambda e, k=k, i=i, rs=rs: e.dma_start(out=wsrc[k][i * rs:(i + 1) * rs, :], in_=w_sh[k][i * rs:(i + 1) * rs, :]),
                 [], [wsb], key='wc' + k)
        S.op('pool', lambda e, k=k: e.collective_compute("AllGather", ALU.bypass, replica_groups=RG, ins=[wsrc[k]], outs=[wfull[k]]),
             [wsb], [wbuf[k]], key='ag' + k, inc=1)

    def run_chunks(chunks, slots, sbs, CW):
        n = len(chunks)
        NS = len(slots)
        st = {'iss': 0}

        def view(i):
            c = chunks[i]
            ncols = sum(p[1] for p in c['pieces'])
            fl = slots[i % NS]
            return bass.AP(fl.tensor, fl.offset, [list(fl.ap[0]), [ncols, c['nkt']], [1, ncols]])

        def issue(i):
            c = chunks[i]
            sl = view(i)
            off = 0
            for (c0, ncl) in c['pieces']:
                src = wfull[c['w']][c['row0']:c['row0'] + c['nkt'] * 128, c0:c0 + ncl].rearrange("(kt p) n -> p kt n", p=128)
                h = c['nkt'] // 2
                for (a, b) in ((0, h), (h, c['nkt'])):
                    S.op('sp', lambda e, sl=sl, src=src, off=off, ncl=ncl, a=a, b=b: e.dma_start(out=sl[:, a:b, off:off + ncl], in_=src[:, a:b, :]),
                         [wbuf[c['w']]], [sbs[i % NS]], key='ws%d' % (i % NS))
                off += ncl

        for i in range(n):
            while st['iss'] < min(n, i + NS):
                issue(st['iss'])
                st['iss'] += 1
            c = chunks[i]
            sl = view(i)
            sb = sbs[i % NS]
            if c.get('pre'):
                c['pre']()
            actT, ab = c['act']
            ncols = sum(p[1] for p in c['pieces'])
            nkt = c['nkt']
            kt0 = c.get('kt0', 0)
            first = c.get('first', True)
            last = c.get('last', True)
            if c['mode'] == 'tok':
                banks = c.get('banks')
                if banks is None:
                    banks = []
                if not banks:
                    banks.extend(c['bank']() for _ in range(4))
                for tb in range(4):
                    bi = banks[tb]
                    for kt in range(nkt):
                        S.op('pe', lambda e, bi=bi, kt=kt, tb=tb, actT=actT, sl=sl, ncols=ncols, nkt=nkt, kt0=kt0, first=first, last=last: e.matmul(
                            psf[bi][:, 0:ncols], actT[:, kt0 + kt, tb * 128:(tb + 1) * 128], sl[:, kt, 0:ncols],
                            start=(first and kt == 0), stop=(last and kt == nkt - 1)),
                            [ab, sb], [pbuf[bi]])
                    if last:
                        c['cb'](tb, bi)
            else:
                for ct in range(ncols // 128):
                    bi = c['bank']()
                    for kt in range(nkt):
                        S.op('pe', lambda e, bi=bi, kt=kt, ct=ct, actT=actT, sl=sl, nkt=nkt: e.matmul(
                            psf[bi][:, 0:512], sl[:, kt, ct * 128:(ct + 1) * 128], actT[:, kt, 0:512], start=(kt == 0), stop=(kt == nkt - 1)),
                            [ab, sb], [pbuf[bi]])
                    c['cb'](ct, bi)
            if c.get('post'):
                c['post']()

    class Ring:
        def __init__(self, lst):
            self.l = lst
            self.i = 0

        def __call__(self):
            v = self.l[self.i % len(self.l)]
            self.i += 1
            return v

    def rstd_from_ss(ss, tmp, rstd, n, rb, wb):
        S.op('act', lambda e: e.activation(out=tmp, in_=ss, func=AF.Ln, scale=1.0 / n, bias=EPS), rb, wb)
        S.op('act', lambda e: e.activation(out=rstd, in_=tmp, func=AF.Exp, scale=-0.5), wb, wb)

    def silu_into(out, src_ps, rb, tmpa, tmpb, tb_):
        S.op('act', lambda e: e.activation(out=tmpa, in_=src_ps, func=AF.Exp, scale=-1.0), rb, [tb_])
        S.op('dve', lambda e: e.tensor_scalar(tmpa, tmpa, 1.0, None, op0=ALU.add), [tb_], [tb_])
        S.op('dve', lambda e: e.reciprocal(out=tmpb, in_=tmpa), [tb_], [tb_])
        return tmpb

    def phaseA(sg):
        if sg == 'p':
            S.barrier()
        ar.off = base_off
        xb = [ar.alloc([128, D], F32) for _ in range(2)]
        xbb = [Buf("xb0"), Buf("xb1")]
        junk = ar.alloc([128, D], BF16); jb = Buf("junk")
        hT = ar.alloc([128, KT, 512], BF16); hTb = Buf("hT")
        slots = [ar.alloc([128, KT * 512], BF16) for _ in range(2)]
        sbs = [Buf("ws0"), Buf("ws1")]
        sm = ar.alloc([128, 8], F32); smb = Buf("sm")
        cosT = ar.alloc([128, 4, 128], F32); sinT = ar.alloc([128, 4, 128], F32); csb = Buf("cs")
        tA = ar.alloc([128, 512], F32); tB = ar.alloc([128, 512], F32); rp = ar.alloc([128, 512], BF16)
        tAb = Buf("tA"); tBb = Buf("tB"); rpb = Buf("rp")
        stg = ar.alloc([128, 4, 512], BF16); stgb = Buf("stg")
        vst = ar.alloc([128, 4, 512], BF16); vstb = Buf("vst")
        g1 = [ar.alloc([128, 512], F32) for _ in range(8)]
        g1b = [Buf("g1_%d" % i) for i in range(8)]
        qdk = ar.alloc([128, 4, 512], BF16); qdkb = Buf("qdk")
        dlst = ar.alloc([128, 2, 8], F32); dlstb = Buf("dlst")
        sc = scr[sg]; sbf = sbufs[sg]
        accr = Ring([0, 1, 2, 3])
        tpr = Ring([4, 5])
        pend = []

        def flush():
            for f_ in pend:
                f_()
            del pend[:]

        for ti in range(NTILE):
            t0 = ti * 512

            def pre(ti=ti, t0=t0):
                ld('sp', cosT, cos_d[sg][t0:t0 + 512, :].rearrange("(tb p) d -> p tb d", p=128), [], [csb], 'cs')
                ld('sp', sinT, sin_d[sg][t0:t0 + 512, :].rearrange("(tb p) d -> p tb d", p=128), [], [csb], 'cs')
                for tb in range(4):
                    x_ = xb[tb % 2]; xb_ = xbb[tb % 2]
                    ld('sp', x_, xin[sg][t0 + tb * 128:t0 + (tb + 1) * 128, :], [], [xb_], 'xb%d' % (tb % 2))
                    S.op('act', lambda e, x_=x_: e.activation(out=junk, in_=x_, func=AF.Square, accum_out=sm[:, 0:1]), [xb_], [jb, smb])
                    rstd_from_ss(sm[:, 0:1], sm[:, 1:2], sm[:, 2:3], D, [smb], [smb])
                    S.op('dve', lambda e, x_=x_: e.tensor_scalar(x_, x_, sm[:, 2:3], None, op0=ALU.mult), [xb_, smb], [xb_])
                    for k4 in range(KT // 4):
                        bi = tpr()
                        for q in range(4):
                            kt = k4 * 4 + q
                            S.op('pe', lambda e, bi=bi, q=q, kt=kt, x_=x_: e.transpose(psf[bi][:, q * 128:(q + 1) * 128], x_[:, kt * 128:(kt + 1) * 128], ident_f),
                                 [xb_, cbuf], [pbuf[bi]])
                        for q in range(4):
                            kt = k4 * 4 + q
                            if q % 2 == 0:
                                S.op('act', lambda e, bi=bi, q=q, kt=kt, tb=tb: e.activation(out=hT[:, kt, tb * 128:(tb + 1) * 128], in_=psf[bi][:, q * 128:(q + 1) * 128],
                                                                                         func=AF.Copy, scale=ancol[:, kt:kt + 1]), [pbuf[bi], cbuf], [hTb])
                            else:
                                S.op('dve', lambda e, bi=bi, q=q, kt=kt, tb=tb: e.tensor_scalar(hT[:, kt, tb * 128:(tb + 1) * 128], psf[bi][:, q * 128:(q + 1) * 128],
                                                                                            ancol[:, kt:kt + 1], None, op0=ALU.mult), [pbuf[bi], cbuf], [hTb])

            chunks = []
            for ci in range(8):
                def cb(tb, bi, ci=ci, t0=t0):
                    flush()
                    ps3 = psf[bi].rearrange("p (g d) -> p g d", g=4)
                    tA3 = tA.rearrange("p (g d) -> p g d", g=4); tB3 = tB.rearrange("p (g d) -> p g d", g=4)
                    S.op('dve', lambda e: e.tensor_tensor(out=tA3, in0=ps3, in1=bmid(cosT[:, tb, :], 4), op=ALU.mult), [pbuf[bi], csb], [tAb])
                    S.op('dve', lambda e: e.tensor_tensor(out=tB3[:, :, 0:64], in0=ps3[:, :, 64:128], in1=bmid(sinT[:, tb, 0:64], 4), op=ALU.mult), [pbuf[bi], csb], [tBb])
                    S.op('dve', lambda e: e.tensor_tensor(out=tB3[:, :, 64:128], in0=ps3[:, :, 0:64], in1=bmid(sinT[:, tb, 64:128], 4), op=ALU.mult), [pbuf[bi], csb], [tBb])
                    S.op('pool', lambda e: e.tensor_tensor(out=rp, in0=tA, in1=tB, op=ALU.add), [tAb, tBb], [rpb])
                    def later(tb=tb, ci=ci, t0=t0):
                        for g in range(4):
                            S.op('pe', lambda e, g=g: e.transpose(psb[:, g * 128:(g + 1) * 128], rp[:, g * 128:(g + 1) * 128], ident_b), [rpb, cbuf], [pbb])
                        S.op('act', lambda e: e.activation(out=stg[:, :, tb * 128:(tb + 1) * 128], in_=psb[:, 0:512].rearrange("p (g t) -> p g t", g=4), func=AF.Copy),
                             [pbb], [stgb])
                        if tb == 3:
                            S.op('pool', lambda e: e.dma_start(out=sc['AQK'][ci * 512:(ci + 1) * 512, t0:t0 + 512].rearrange("(g p) t -> p g t", p=128), in_=stg),
                                 [stgb], [sbf['AQK']], key='st_aqk')
                    later()
                chunks.append({'w': 'in', 'row0': 0, 'nkt': KT, 'pieces': [(ci * 512, 512)], 'mode': 'tok', 'act': (hT, hTb), 'bank': accr, 'cb': cb})

            def mk_tok(colbase, dst, kind):
                for ci in range(4):
                    def cb(tb, bi, ci=ci, t0=t0, dst=dst, kind=kind):
                        flush()
                        if kind == 'copy':
                            S.op('act', lambda e: e.activation(out=vst[:, tb, :], in_=psf[bi], func=AF.Copy), [pbuf[bi]], [vstb])
                        else:
                            r = silu_into(None, psf[bi], [pbuf[bi]], tA, tB, tAb)
                            S.op('dve', lambda e: e.tensor_tensor(out=vst[:, tb, :], in0=psf[bi], in1=r, op=ALU.mult), [pbuf[bi], tAb], [vstb])
                        if tb == 3:
                            S.op('pool', lambda e: e.dma_start(out=sc[dst][t0:t0 + 512, ci * 512:(ci + 1) * 512].rearrange("(tb p) c -> p tb c", p=128), in_=vst),
                                 [vstb], [sbf[dst]], key='st_' + dst)
                    chunks.append({'w': 'in', 'row0': 0, 'nkt': KT, 'pieces': [(colbase + ci * 512, 512)], 'mode': 'tok', 'act': (hT, hTb), 'bank': accr, 'cb': cb})
            mk_tok(4096, 'AV', 'copy')
            mk_tok(12288, 'HV', 'copy')
            mk_tok(14336, 'HG', 'silu')

            for j in range(16):
                hold = {}

                def cb(ct, bi, j=j, t0=t0, ti=ti, hold=hold):
                    hold[ct] = bi
                    if ct < 2:
                        return
                    bq, bfw, bbw = hold[0], hold[1], hold[2]
                    G = g1; Gb = g1b
                    r = silu_into(None, psf[bq], [pbuf[bq]], G[1], G[2], Gb[1])
                    S.op('dve', lambda e: e.tensor_tensor(out=G[0], in0=psf[bq], in1=r, op=ALU.mult), [pbuf[bq], Gb[1]], [Gb[0]])
                    for d, bk in ((0, bfw), (1, bbw)):
                        lbc = lb[:, d * 16 + j:d * 16 + j + 1]; omc = oml[:, d * 16 + j:d * 16 + j + 1]
                        S.op('act', lambda e, bk=bk: e.activation(out=G[1], in_=psf[bk], func=AF.Exp, scale=-1.0), [pbuf[bk]], [Gb[1]])
                        S.op('dve', lambda e: e.tensor_scalar(G[1], G[1], 1.0, None, op0=ALU.add), [Gb[1]], [Gb[1]])
                        S.op('dve', lambda e: e.reciprocal(out=G[2], in_=G[1]), [Gb[1]], [Gb[2]])
                        S.op('dve', lambda e, lbc=lbc, omc=omc: e.tensor_scalar(G[3], G[2], omc, lbc, op0=ALU.mult, op1=ALU.add), [Gb[2], cbuf], [Gb[3]])
                        S.op('act', lambda e: e.activation(out=G[4], in_=G[3], func=AF.Ln), [Gb[3]], [Gb[4]])
                        S.op('pool', lambda e: e.tensor_scalar(G[3], G[3], -1.0, 1.0, op0=ALU.mult, op1=ALU.add), [Gb[3]], [Gb[3]])
                        S.op('dve', lambda e: e.tensor_tensor_scan(out=G[5], data0=scanm, data1=G[4], initial=0.0, op0=ALU.mult, op1=ALU.add), [Gb[4], cbuf], [Gb[5]])
                        if d == 0:
                            bsrc = G[5]; bb_ = Gb[5]
                        else:
                            S.op('dve', lambda e: e.tensor_tensor(out=G[4], in0=G[4], in1=G[5], op=ALU.subtract), [Gb[4], Gb[5]], [Gb[4]])
                            c3 = G[5].rearrange("p (c t) -> p c t", c=8)
                            S.op('dve', lambda e, c3=c3: e.tensor_tensor(out=G[4].rearrange("p (c t) -> p c t", c=8), in0=G[4].rearrange("p (c t) -> p c t", c=8),
                                                                        in1=blast(c3[:, :, 63], 64), op=ALU.add), [Gb[4], Gb[5]], [Gb[4]])
                            bsrc = G[4]; bb_ = Gb[4]
                        S.op('act', lambda e, bsrc=bsrc: e.activation(out=G[6], in_=bsrc, func=AF.Exp), [bb_], [Gb[6]])
                        S.op('act', lambda e, bsrc=bsrc: e.activation(out=G[7], in_=bsrc, func=AF.Exp, scale=-1.0), [bb_], [Gb[7]])
                        S.op('dve', lambda e, d=d: e.tensor_tensor(out=qdk[:, d, :], in0=G[0], in1=G[6], op=ALU.mult), [Gb[0], Gb[6]], [qdkb])
                        S.op('pool', lambda e, d=d: e.tensor_tensor(out=qdk[:, 2 + d, :], in0=G[3], in1=G[7], op=ALU.mult), [Gb[3], Gb[7]], [qdkb])
                        e3 = G[6].rearrange("p (c t) -> p c t", c=8)
                        S.op('act', lambda e, d=d, e3=e3: e.activation(out=dlst[:, d, :], in_=e3[:, :, 63 if d == 0 else 0], func=AF.Copy), [Gb[6]], [dlstb])
                    for kind in range(2):
                        S.op('pool', lambda e, kind=kind: e.dma_start(
                            out=sc['HQK'][kind * 4096 + j * 256:kind * 4096 + (j + 1) * 256, t0:t0 + 512].rearrange("(d p) t -> p d t", p=128),
                            in_=qdk[:, 2 * kind:2 * kind + 2, :]), [qdkb], [sbf['HQK']], key='st_hqk')
                    S.op('pool', lambda e: e.dma_start(out=sc['HD'][j * 256:(j + 1) * 256, ti * 8:(ti + 1) * 8].rearrange("(d p) c -> p d c", p=128), in_=dlst),
                         [dlstb], [sbf['HD']], key='st_hd')
                chunks.append({'w': 'in', 'row0': 0, 'nkt': KT, 'pieces': [(6144 + j * 128, 128), (8192 + j * 128, 128), (10240 + j * 128, 128)],
                               'mode': 'feat', 'act': (hT, hTb), 'bank': accr, 'cb': cb})
            chunks[0]['pre'] = pre
            chunks[-1]['post'] = flush
            run_chunks(chunks, slots, sbs, 512)

    def phaseB_attn(sg):
        S.barrier()
        L = SEG if sg == 'p' else LS
        NKB = L // 128
        NP = 1 if sg == 'p' else 8
        ar.off = base_off
        KTs = ar.alloc([128, 2, L], BF16); V = ar.alloc([128, NKB, 256], BF16)
        kvb = Buf("kv")
        QT = [ar.alloc([128, 2, 512], BF16) for _ in range(2)]; qb = [Buf("q0"), Buf("q1")]
        P = [ar.alloc([128, 512], BF16) for _ in range(3)]; Pb = [Buf("P%d" % i) for i in range(3)]
        r0 = ar.alloc([128, 512], F32); r1 = ar.alloc([128, 512], F32); rb_ = Buf("r01")
        t0_ = ar.alloc([128, 512], F32); t1_ = ar.alloc([128, 512], F32); tb_ = Buf("t01")
        o_ = [ar.alloc([128, 512], F32) for _ in range(2)]; ob_ = Buf("o")
        sq = [ar.alloc([128, 512], F32) for _ in range(2)]; sqb = Buf("sq")
        rs = ar.alloc([128, 512], F32); rsb = Buf("rs")
        on = ar.alloc([128, 2, 512], BF16); onb = Buf("on")
        sacc = [[ar.alloc([128, 512], F32) for _ in range(2)] for _ in range(2)]
        saccb = [[Buf("sacc%d%d" % (c_, k_)) for k_ in range(2)] for c_ in range(2)]
        src = scr['p'] if sg == 'p' else gat
        srb = sbufs['p'] if sg == 'p' else gbufs
        heads = range(8) if sg == 'p' else [None]
        SB = [0, 1]
        ACC = [[2, 3, 4], [5, 6, 7]]
        qi = 0
        pi = 0
        sc_ = 1.0 / math.sqrt(128.0)
        for h in heads:
            for r in range(NP):
                for c in range(2):
                    if sg == 'p':
                        s_q = src['AQK'][2048 + (h * 2 + c) * 128:2048 + (h * 2 + c + 1) * 128, :]
                        ld('sp', KTs[:, c, r * SEG:(r + 1) * SEG], s_q, [srb['AQK']], [kvb], 'kv')
                    else:
                        ldd(KTs[:, c, r * SEG:(r + 1) * SEG], lambda core, r=r, c=c: src['AQK'][core * 256 + r * 4096 + 2048 + c * 128:core * 256 + r * 4096 + 2048 + c * 128 + 128, :],
                            [srb['AQK']], [kvb], 'kv')
                av3 = src['AV'].rearrange("(kb p) e -> p kb e", p=128)
                if sg == 'p':
                    s_v = av3[:, :, h * 256:(h + 1) * 256]
                else:
                    ldd(V[:, r * (SEG // 128):(r + 1) * (SEG // 128), :], lambda core, r=r: av3[:, r * (SEG // 128):(r + 1) * (SEG // 128), core * 256:(core + 1) * 256],
                        [srb['AV']], [kvb], 'kv')
                    continue
                ld('sp', V[:, r * (SEG // 128):(r + 1) * (SEG // 128), :], s_v, [srb['AV']], [kvb], 'kv')
            for qt in range(L // 512):
                q_ = QT[qi % 2]; qb_ = qb[qi % 2]; qi += 1
                r = (qt * 512) // SEG; lo = (qt * 512) % SEG
                for c in range(2):
                    if sg == 'p':
                        s_q = src['AQK'][(h * 2 + c) * 128:(h * 2 + c + 1) * 128, lo:lo + 512]
                        ld('sp', q_[:, c, :], s_q, [srb['AQK']], [qb_], 'q%d' % ((qi - 1) % 2))
                    else:
                        ldd(q_[:, c, :], lambda core, r=r, c=c, lo=lo: src['AQK'][core * 256 + r * 4096 + c * 128:core * 256 + r * 4096 + c * 128 + 128, lo:lo + 512],
                            [srb['AQK']], [qb_], 'q%d' % ((qi - 1) % 2))
                for c in range(2):
                    acc = ACC[c]

                    def sbank(kb, c=c):
                        return kb % 2

                    def emit_s(kb, c=c, q_=q_, qb_=qb_):
                        bi = sbank(kb)
                        S.op('pe', lambda e, bi=bi, kb=kb: e.matmul(psf[bi], KTs[:, c, kb * 128:(kb + 1) * 128], q_[:, c, :], start=True, stop=True),
                             [kvb, qb_], [pbuf[bi]])
                    emit_s(0)
                    for kb in range(NKB):
                        p_ = P[pi % 3]; pb_ = Pb[pi % 3]; pi += 1
                        bi = sbank(kb)
                        S.op('act', lambda e, bi=bi, p_=p_: e.activation(out=p_, in_=psf[bi], func=AF.Exp, scale=sc_), [pbuf[bi]], [pb_])
                        if kb + 1 < NKB:
                            emit_s(kb + 1)
                        for a, lhs in ((0, V[:, kb, 0:128]), (1, V[:, kb, 128:256])):
                            S.op('pe', lambda e, a=a, lhs=lhs, p_=p_, kb=kb, acc=acc: e.matmul(psf[acc[a]], lhs, p_, start=(kb == 0), stop=(kb == NKB - 1)),
                                 [kvb, pb_, cbuf], [pbuf[acc[a]]])
                        eng_ = 'dve' if kb % 2 == 0 else 'pool'
                        sa = sacc[c][kb % 2]; sab = saccb[c][kb % 2]
                        if kb < 2:
                            S.op(eng_, lambda e, sa=sa, p_=p_: e.tensor_copy(out=sa, in_=p_), [pb_], [sab])
                        else:
                            S.op(eng_, lambda e, sa=sa, p_=p_: e.tensor_tensor(out=sa, in0=sa, in1=p_, op=ALU.add), [pb_, sab], [sab])
                    for k_ in range(2):
                        S.op('pe', lambda e, k_=k_, c=c, acc=acc: e.matmul(psf[acc[2]], ones_f, sacc[c][k_], start=(k_ == 0), stop=(k_ == 1)),
                             [saccb[c][k_], cbuf], [pbuf[acc[2]]])
                A0, A1 = ACC
                S.op('dve', lambda e: e.reciprocal(out=r0, in_=psf[A0[2]]), [pbuf[A0[2]]], [rb_])
                S.op('dve', lambda e: e.reciprocal(out=r1, in_=psf[A1[2]]), [pbuf[A1[2]]], [rb_])
                S.op('dve', lambda e: e.tensor_scalar(r1, r1, nlam[:, 0:1], None, op0=ALU.mult), [rb_, cbuf], [rb_])
                for hf in range(2):
                    S.op('dve', lambda e, hf=hf: e.tensor_tensor(out=t0_, in0=psf[A0[hf]], in1=r0, op=ALU.mult), [pbuf[A0[hf]], rb_], [tb_])
                    S.op('dve', lambda e, hf=hf: e.tensor_tensor(out=t1_, in0=psf[A1[hf]], in1=r1, op=ALU.mult), [pbuf[A1[hf]], rb_], [tb_])
                    S.op('pool', lambda e, hf=hf: e.tensor_tensor(out=o_[hf], in0=t0_, in1=t1_, op=ALU.add), [tb_], [ob_])
                    S.op('act', lambda e, hf=hf: e.activation(out=sq[hf], in_=o_[hf], func=AF.Square), [ob_], [sqb])
                for hf in range(2):
                    S.op('pe', lambda e, hf=hf: e.matmul(psf[0], ones_f, sq[hf], start=(hf == 0), stop=(hf == 1)), [sqb, cbuf], [pbuf[0]])
                S.op('act', lambda e: e.activation(out=rs, in_=psf[0], func=AF.Ln, scale=1.0 / 256.0, bias=EPS), [pbuf[0]], [rsb])
                S.op('act', lambda e: e.activation(out=rs, in_=rs, func=AF.Exp, scale=-0.5), [rsb], [rsb])
                for hf in range(2):
                    S.op('dve', lambda e, hf=hf: e.scalar_tensor_tensor(out=on[:, hf, :], in0=o_[hf], scalar=subcol[:, hf:hf + 1], in1=rs, op0=ALU.mult, op1=ALU.mult),
                         [ob_, rsb, cbuf], [onb])
                if sg == 'p' and h == 0 and qt == 0:
                    dump("r0", r0, [rb_], [128, 512]); dump("r1", r1, [rb_], [128, 512]); dump("t0", t0_, [tb_], [128, 512])
                    dump("o0", o_[0], [ob_], [128, 512]); dump("sq0", sq[0], [sqb], [128, 512]); dump("rs", rs, [rsb], [128, 512])
                    dump("P", P[(pi - 1) % 3], [Pb[(pi - 1) % 3]], [128, 512], BF16); dump("nlam", nlam, [cbuf], [128, 1])
                    dump("acc00", psf[2], [pbuf[2]], [128, 512]) if False else None
                if sg == 'p':
                    S.op('pool', lambda e, h=h, qt=qt: e.dma_start(out=OT_P[h * 256:(h + 1) * 256, qt * 512:(qt + 1) * 512].rearrange("(f p) t -> p f t", p=128), in_=on),
                         [onb], [b_OT_P], key='st_ot')
                else:
                    S.op('pool', lambda e, qt=qt: e.dma_start(out=OTa_s[:, qt * 512:(qt + 1) * 512].rearrange("(f p) t -> p f t", p=128), in_=on),
                         [onb], [b_OTa_s], key='st_ota')

    def phaseB_hgrn(sg):
        S.barrier()
        L = SEG if sg == 'p' else LS
        NT = L // 512
        ar.off = base_off
        src = scr['p'] if sg == 'p' else gat
        srb = sbufs['p'] if sg == 'p' else gbufs
        OBW = OBW_P if sg == 'p' else OBW_S
        HP = 2
        Sf = [ar.alloc([128, 9, 128], F32) for _ in range(HP)]; Sfb = [Buf("Sf%d" % i) for i in range(HP)]
        Sall = [ar.alloc([128, 8, 128], BF16) for _ in range(HP)]; Sab = [Buf("Sall%d" % i) for i in range(HP)]
        Ud = [ar.alloc([128, 8, 128], F32) for _ in range(HP)]; Udb = [Buf("Ud%d" % i) for i in range(HP)]
        QK = [[ar.alloc([128, 2, 512], BF16) for _ in range(2)] for _ in range(HP)]
        QKb = [[Buf("qk%d%d" % (i, k)) for k in range(2)] for i in range(HP)]
        VH = [[ar.alloc([64, 8, 128], BF16) for _ in range(2)] for _ in range(HP)]
        VHb = [[Buf("vh%d%d" % (i, k)) for k in range(2)] for i in range(HP)]
        DLt = [[ar.alloc([128, 8], F32) for _ in range(2)] for _ in range(HP)]
        DLb = [[Buf("dl%d%d" % (i, k)) for k in range(2)] for i in range(HP)]
        ATm = [ar.alloc([64, 8, 64], BF16) for _ in range(HP)]; ATb = [Buf("at%d" % i) for i in range(HP)]
        kdt = [ar.alloc([64, 8, 128], BF16) for _ in range(HP)]; kdb = [Buf("kdt%d" % i) for i in range(HP)]
        ostg = [[ar.alloc([64, 8, 128], F32) for _ in range(2)] for _ in range(HP)]
        ostb = [[Buf("os%d%d" % (i, k)) for k in range(2)] for i in range(HP)]
        OG = [[ar.alloc([64, 8, 128], BF16) for _ in range(2)] for _ in range(HP)]
        OGb = [[Buf("og%d%d" % (i, k)) for k in range(2)] for i in range(HP)]
        sqt = [ar.alloc([64, 8, 128], F32) for _ in range(HP)]; sqb = [Buf("sqt%d" % i) for i in range(HP)]
        ssn = [ar.alloc([64, 24], F32) for _ in range(HP)]; ssb = [Buf("ssn%d" % i) for i in range(HP)]
        onb16 = [ar.alloc([64, 8, 128], BF16) for _ in range(HP)]; onb = [Buf("on%d" % i) for i in range(HP)]
        otst = [ar.alloc([128, 512], BF16) for _ in range(HP)]; otb = [Buf("ot%d" % i) for i in range(HP)]
        PB = [[0, 1, 2], [3, 4, 5]]
        groups = [[2 * i, 2 * i + 1] for i in range(8)] if sg == 'p' else [[0, 1]]
        cnt = {'t': 0}
        for grp in groups:
            for d in (1, 0):
                for hp in range(HP):
                    S.op('dve', lambda e, hp=hp: e.memset(Sf[hp][:, 0, :], 0.0), [], [Sfb[hp]])
                tiles = range(NT) if d == 0 else range(NT - 1, -1, -1)
                for tl in tiles:
                    par = cnt['t'] % 2
                    cnt['t'] += 1
                    r = (tl * 512) // SEG; lo = (tl * 512) % SEG
                    for hp, j in enumerate(grp):
                        for kind in range(2):
                            if sg == 'p':
                                s_ = src['HQK'][kind * 4096 + (j * 2 + d) * 128:kind * 4096 + (j * 2 + d + 1) * 128, lo:lo + 512]
                                ld('sp', QK[hp][par][:, kind, :], s_, [srb['HQK']], [QKb[hp][par]], 'hqk%d%d' % (hp, par))
                            else:
                                def f_(core, r=r, kind=kind, j=j, d=d, lo=lo):
                                    b0 = core * 512 + r * 8192 + kind * 4096 + (j * 2 + d) * 128
                                    return src['HQK'][b0:b0 + 128, lo:lo + 512]
                                ldd(QK[hp][par][:, kind, :], f_, [srb['HQK']], [QKb[hp][par]], 'hqk%d%d' % (hp, par))
                        if sg == 'p':
                            s_v = src['HV'].rearrange("(c p) e -> p c e", p=64)[:, tl * 8:(tl + 1) * 8, j * 128:(j + 1) * 128]
                            s_g = src['HG'].rearrange("(c p) e -> p c e", p=64)[:, tl * 8:(tl + 1) * 8, j * 128:(j + 1) * 128]
                            s_d = src['HD'][(j * 2 + d) * 128:(j * 2 + d + 1) * 128, tl * 8:(tl + 1) * 8]
                            obw = OBW[j * SEG + tl * 512:j * SEG + (tl + 1) * 512, :]
                        else:
                            s_v = None
                            s_g = None
                            s_d = None
                            obw = OBW[j * LS + tl * 512:j * LS + (tl + 1) * 512, :]
                        if sg == 'p':
                            ld('sp', VH[hp][par], s_v, [srb['HV']], [VHb[hp][par]], 'hv%d%d' % (hp, par))
                        else:
                            ldd(VH[hp][par], lambda core, tl=tl, j=j: src['HV'].rearrange("(c p) e -> p c e", p=64)[:, tl * 8:(tl + 1) * 8, core * 256 + j * 128:core * 256 + (j + 1) * 128],
                                [srb['HV']], [VHb[hp][par]], 'hv%d%d' % (hp, par))
                        if sg == 'p':
                            ld('sp', DLt[hp][par], s_d, [srb['HD']], [DLb[hp][par]], 'hd%d%d' % (hp, par))
                        else:
                            def fd_(core, r=r, j=j, d=d, lo=lo):
                                b0 = core * 512 + r * 4096 + (j * 2 + d) * 128
                                return src['HD'][b0:b0 + 128, (lo // 64):(lo // 64) + 8]
                            ldd(DLt[hp][par], fd_, [srb['HD']], [DLb[hp][par]], 'hd%d%d' % (hp, par))
                        if d == 0:
                            if sg == 'p':
                                ld('sp', OG[hp][par], s_g, [srb['HG']], [OGb[hp][par]], 'hg%d%d' % (hp, par))
                            else:
                                ldd(OG[hp][par], lambda core, tl=tl, j=j: src['HG'].rearrange("(c p) e -> p c e", p=64)[:, tl * 8:(tl + 1) * 8, core * 256 + j * 128:core * 256 + (j + 1) * 128],
                                    [srb['HG']], [OGb[hp][par]], 'hg%d%d' % (hp, par))
                            ld('sp', ostg[hp][par], obw.rearrange("(c p) e -> p c e", p=64), [b_OBW], [ostb[hp][par]], 'ob%d%d' % (hp, par))
                    corder = list(range(8)) if d == 0 else list(range(7, -1, -1))
                    UB = [[1, 2], [3, 4]]
                    for hp, j in enumerate(grp):
                        qkb = QKb[hp][par]; vb = VHb[hp][par]
                        ub = UB[hp]
                        for c in range(8):
                            qd = QK[hp][par][:, 0, c * 64:(c + 1) * 64]; kd = QK[hp][par][:, 1, c * 64:(c + 1) * 64]
                            S.op('pe', lambda e, c=c, kd=kd, qd=qd: e.matmul(psf[0][0:64, c * 64:(c + 1) * 64], kd, qd, start=True, stop=True), [qkb], [pbuf[0]])
                        for c in range(8):
                            kd = QK[hp][par][:, 1, c * 64:(c + 1) * 64]
                            S.op('pe', lambda e, c=c, kd=kd: e.transpose(psb[0:64, c * 128:(c + 1) * 128], kd, ident_b), [qkb, cbuf], [pbb])
                        S.op('dve', lambda e, hp=hp, d=d: e.tensor_tensor(out=ATm[hp], in0=psf[0][0:64, :].rearrange("p (c t) -> p c t", c=8),
                                                                      in1=bmid(masks[:, d * 64:(d + 1) * 64], 8), op=ALU.mult), [pbuf[0], cbuf], [ATb[hp]])
                        S.op('act', lambda e, hp=hp: e.activation(out=kdt[hp], in_=psb[0:64, :].rearrange("p (c k) -> p c k", c=8), func=AF.Copy), [pbb], [kdb[hp]])
                        for c in range(8):
                            bi = ub[c // 4]
                            S.op('pe', lambda e, bi=bi, c=c, hp=hp, par=par: e.matmul(psf[bi][:, (c % 4) * 128:(c % 4 + 1) * 128], kdt[hp][:, c, :], VH[hp][par][:, c, :], start=True, stop=True),
                                 [kdb[hp], vb], [pbuf[bi]])
                        for hf in range(2):
                            bi = ub[hf]
                            S.op('dve', lambda e, bi=bi, hf=hf, hp=hp, par=par: e.tensor_tensor(out=Ud[hp][:, hf * 4:(hf + 1) * 4, :], in0=psf[bi].rearrange("p (c k) -> p c k", c=4),
                                                                                            in1=blast(DLt[hp][par][:, hf * 4:(hf + 1) * 4], 128), op=ALU.mult),
                                 [pbuf[bi], DLb[hp][par]], [Udb[hp]])
                    for hp, j in enumerate(grp):
                        for i, c in enumerate(corder):
                            S.op('dve', lambda e, hp=hp, i=i, c=c, par=par: e.scalar_tensor_tensor(out=Sf[hp][:, i + 1, :], in0=Sf[hp][:, i, :], scalar=DLt[hp][par][:, c:c + 1],
                                                                                               in1=Ud[hp][:, c, :], op0=ALU.mult, op1=ALU.add),
                                 [Sfb[hp], Udb[hp], DLb[hp][par]], [Sfb[hp]])
                        S.op('act', lambda e, hp=hp: e.activation(out=Sall[hp], in_=Sf[hp][:, 0:8, :], func=AF.Copy), [Sfb[hp]], [Sab[hp]])
                        S.op('pool', lambda e, hp=hp: e.tensor_copy(out=Sf[hp][:, 0, :], in_=Sf[hp][:, 8, :]), [Sfb[hp], Sab[hp]], [Sfb[hp]])
                    for hp, j in enumerate(grp):
                        qkb = QKb[hp][par]; vb = VHb[hp][par]
                        for i, c in enumerate(corder):
                            bi = 5 + c // 4
                            qd = QK[hp][par][:, 0, c * 64:(c + 1) * 64]
                            S.op('pe', lambda e, bi=bi, c=c, hp=hp, par=par: e.matmul(psf[bi][0:64, (c % 4) * 128:(c % 4 + 1) * 128], ATm[hp][:, c, :], VH[hp][par][:, c, :], start=True, stop=False),
                                 [ATb[hp], vb], [pbuf[bi]])
                            S.op('pe', lambda e, bi=bi, c=c, hp=hp, i=i, qd=qd: e.matmul(psf[bi][0:64, (c % 4) * 128:(c % 4 + 1) * 128], qd, Sall[hp][:, i, :], start=False, stop=True),
                                 [qkb, Sab[hp]], [pbuf[bi]])
                        for hf in range(2):
                            bi = 5 + hf
                            if d == 1:
                                S.op('act', lambda e, bi=bi, hf=hf, hp=hp, par=par: e.activation(out=ostg[hp][par][:, hf * 4:(hf + 1) * 4, :], in_=psf[bi][0:64, :].rearrange("p (c k) -> p c k", c=4),
                                                                                             func=AF.Copy), [pbuf[bi]], [ostb[hp][par]])
                            else:
                                S.op('dve', lambda e, bi=bi, hf=hf, hp=hp, par=par: e.tensor_tensor(out=ostg[hp][par][:, hf * 4:(hf + 1) * 4, :], in0=psf[bi][0:64, :].rearrange("p (c k) -> p c k", c=4),
                                                                                                in1=ostg[hp][par][:, hf * 4:(hf + 1) * 4, :], op=ALU.add), [pbuf[bi], ostb[hp][par]], [ostb[hp][par]])
                    for hp, j in enumerate(grp):
                        if sg == 'p':
                            obw = OBW[j * SEG + tl * 512:j * SEG + (tl + 1) * 512, :]
                        else:
                            obw = OBW[j * LS + tl * 512:j * LS + (tl + 1) * 512, :]
                        os_ = ostg[hp][par]; osb_ = ostb[hp][par]
                        if d == 1:
                            S.op('pool', lambda e, obw=obw, os_=os_: e.dma_start(out=obw.rearrange("(c p) e -> p c e", p=64), in_=os_), [osb_], [b_OBW], key='st_obw')
                            continue
                        S.op('act', lambda e, hp=hp, os_=os_: e.activation(out=sqt[hp], in_=os_, func=AF.Square), [osb_], [sqb[hp]])
                        S.op('dve', lambda e, hp=hp: e.tensor_reduce(out=ssn[hp][:, 0:8], in_=sqt[hp], axis=AX.X, op=ALU.add), [sqb[hp]], [ssb[hp]])
                        rstd_from_ss(ssn[hp][:, 0:8], ssn[hp][:, 8:16], ssn[hp][:, 16:24], 128, [ssb[hp]], [ssb[hp]])
                        S.op('dve', lambda e, hp=hp, os_=os_: e.tensor_tensor(out=sqt[hp], in0=os_, in1=blast(ssn[hp][:, 16:24], 128), op=ALU.mult), [osb_, ssb[hp]], [sqb[hp]])
                        S.op('dve', lambda e, hp=hp: e.tensor_tensor(out=sqt[hp], in0=sqt[hp], in1=bmid(hnw, 8), op=ALU.mult), [sqb[hp], cbuf], [sqb[hp]])
                        S.op('dve', lambda e, hp=hp, par=par: e.tensor_tensor(out=onb16[hp], in0=sqt[hp], in1=OG[hp][par], op=ALU.mult), [sqb[hp], OGb[hp][par]], [onb[hp]])
                        for c in range(8):
                            S.op('pe', lambda e, hp=hp, c=c: e.transpose(psb[:, 512 + c * 64:512 + (c + 1) * 64], onb16[hp][:, c, :], ident_b[0:64, 0:64]), [onb[hp], cbuf], [pbb])
                        S.op('act', lambda e, hp=hp: e.activation(out=otst[hp], in_=psb[:, 512:1024], func=AF.Copy), [pbb], [otb[hp]])
                        if sg == 'p':
                            S.op('pool', lambda e, hp=hp, j=j, tl=tl: e.dma_start(out=OT_P[2048 + j * 128:2048 + (j + 1) * 128, tl * 512:(tl + 1) * 512], in_=otst[hp]),
                                 [otb[hp]], [b_OT_P], key='st_ot')
                        else:
                            S.op('pool', lambda e, hp=hp, j=j, tl=tl: e.dma_start(out=OTh_s[j * 128:(j + 1) * 128, tl * 512:(tl + 1) * 512], in_=otst[hp]),
                                 [otb[hp]], [b_OTh_s], key='st_oth')

    def phaseC(sg):
        S.barrier()
        ar.off = base_off
        oT = ar.alloc([128, KT, 512], BF16); oTb = Buf("oT")
        x1 = ar.alloc([128, 4, D], F32); x1b = [Buf("x1_%d" % i) for i in range(4)]
        aT_off = ar.off
        aT = ar.alloc([128, KT, 512], BF16); aTb = Buf("aT")
        hnb = alloc_at(aT_off, [128, D], BF16)
        fw = alloc_at(aT_off + 8192, [128, D], F32)
        slots = [ar.alloc([128, 8192], BF16) for _ in range(3)]
        sbs = [Buf("wc0"), Buf("wc1"), Buf("wc2")]
        xr = [ar.alloc([128, 4, 512], F32) for _ in range(2)]; xrb = [Buf("xr0"), Buf("xr1")]
        rl = [ar.alloc([128, 512], F32) for _ in range(2)]; rlb = [Buf("rl0"), Buf("rl1")]
        sm = ar.alloc([128, 8], F32); smb = Buf("smc")
        accr = Ring([0, 1, 2, 3])
        xi = {'i': 0}
        ri = {'i': 0}
        for ti in range(NTILE):
            t0 = ti * 512

            def pre(t0=t0):
                if sg == 'p':
                    ld('sp', oT, OT_P[:, t0:t0 + 512].rearrange("(kt p) t -> p kt t", p=128), [b_OT_P], [oTb], 'oT')
                else:
                    ldd(oT[:, 0:16, :], lambda core, t0=t0: G_OTa[:, core * SEG + t0:core * SEG + t0 + 512].rearrange("(kt p) t -> p kt t", p=128), [b_GOTa], [oTb], 'oT')
                    ldd(oT[:, 16:32, :], lambda core, t0=t0: G_OTh[:, core * SEG + t0:core * SEG + t0 + 512].rearrange("(kt p) t -> p kt t", p=128), [b_GOTh], [oTb], 'oT')

            chunks = []
            for ci in range(8):
                st = {}
                bk = []

                def cpre(ci=ci, t0=t0, st=st):
                    k = xi['i'] % 2
                    xi['i'] += 1
                    st['k'] = k
                    ld('sp', xr[k], xin[sg][t0:t0 + 512, ci * 512:(ci + 1) * 512].rearrange("(tb p) c -> p tb c", p=128), [], [xrb[k]], 'xr%d' % k)

                def cb(tb, bi, ci=ci, st=st):
                    k = st['k']
                    S.op('dve', lambda e: e.tensor_tensor(out=x1[:, tb, ci * 512:(ci + 1) * 512], in0=psf[bi][:, 0:512], in1=xr[k][:, tb, :], op=ALU.add),
                         [pbuf[bi], xrb[k]], [x1b[tb]])
                chunks.append({'w': 'out', 'row0': 0, 'nkt': 16, 'kt0': 0, 'first': True, 'last': False, 'banks': bk, 'pieces': [(ci * 512, 512)],
                               'mode': 'tok', 'act': (oT, oTb), 'bank': accr, 'cb': cb, 'pre': cpre})
                chunks.append({'w': 'out', 'row0': 2048, 'nkt': 16, 'kt0': 16, 'first': False, 'last': True, 'banks': bk, 'pieces': [(ci * 512, 512)],
                               'mode': 'tok', 'act': (oT, oTb), 'bank': accr, 'cb': cb})
            first_pre = chunks[0]['pre']

            def pre0(first_pre=first_pre, pre=pre):
                pre()
                first_pre()
            chunks[0]['pre'] = pre0

            def norm2():
                for tb in range(4):
                    S.op('act', lambda e, tb=tb: e.activation(out=hnb, in_=x1[:, tb, :], func=AF.Square, accum_out=sm[:, 0:1]), [x1b[tb]], [aTb, smb])
                    rstd_from_ss(sm[:, 0:1], sm[:, 1:2], sm[:, 2:3], D, [smb], [smb])
                    S.op('dve', lambda e, tb=tb: e.tensor_scalar(hnb, x1[:, tb, :], sm[:, 2:3], None, op0=ALU.mult), [x1b[tb], smb], [aTb])
                    for k4 in range(KT // 4):
                        for q in range(4):
                            kt = k4 * 4 + q
                            S.op('pe', lambda e, q=q, kt=kt: e.transpose(psb[:, q * 128:(q + 1) * 128], hnb[:, kt * 128:(kt + 1) * 128], ident_b), [aTb, cbuf], [pbb])
                        for q in range(4):
                            kt = k4 * 4 + q
                            if q % 2 == 0:
                                S.op('act', lambda e, q=q, kt=kt, tb=tb: e.activation(out=oT[:, kt, tb * 128:(tb + 1) * 128], in_=psb[:, q * 128:(q + 1) * 128], func=AF.Copy,
                                                                                    scale=mncol[:, kt:kt + 1]), [pbb, cbuf], [oTb])
                            else:
                                S.op('dve', lambda e, q=q, kt=kt, tb=tb: e.tensor_scalar(oT[:, kt, tb * 128:(tb + 1) * 128], psb[:, q * 128:(q + 1) * 128], mncol[:, kt:kt + 1], None,
                                                                                       op0=ALU.mult), [pbb, cbuf], [oTb])
            chunks[-1]['post'] = norm2

            for g in range(NF // 4096):
                for ci in range(16):
                    def cb(ct, bi, ci=ci):
                        k = ri['i'] % 2
                        ri['i'] += 1
                        S.op('act', lambda e: e.activation(out=rl[k], in_=psf[bi], func=AF.Relu), [pbuf[bi]], [rlb[k]])
                        S.op('dve', lambda e: e.tensor_tensor(out=aT[:, ci * 2 + ct, :], in0=rl[k], in1=rl[k], op=ALU.mult), [rlb[k]], [aTb])
                    chunks.append({'w': 'up', 'row0': 0, 'nkt': KT, 'pieces': [(g * 4096 + ci * 256, 256)], 'mode': 'feat', 'act': (oT, oTb), 'bank': accr, 'cb': cb})
                for ci in range(8):
                    bk = []

                    def cb(tb, bi, ci=ci):
                        S.op('dve', lambda e: e.tensor_tensor(out=x1[:, tb, ci * 512:(ci + 1) * 512], in0=psf[bi][:, 0:512], in1=x1[:, tb, ci * 512:(ci + 1) * 512], op=ALU.add),
                             [pbuf[bi], x1b[tb]], [x1b[tb]])
                    chunks.append({'w': 'down', 'row0': g * 4096, 'nkt': 16, 'kt0': 0, 'first': True, 'last': False, 'banks': bk, 'pieces': [(ci * 512, 512)],
                                   'mode': 'tok', 'act': (aT, aTb), 'bank': accr, 'cb': cb})
                    chunks.append({'w': 'down', 'row0': g * 4096 + 2048, 'nkt': 16, 'kt0': 16, 'first': False, 'last': True, 'banks': bk, 'pieces': [(ci * 512, 512)],
                                   'mode': 'tok', 'act': (aT, aTb), 'bank': accr, 'cb': cb})

            def fin(t0=t0):
                ld('sp', fw, finw_d.partition_broadcast(128), [], [aTb], 'finw')
                for tb in range(4):
                    S.op('act', lambda e, tb=tb: e.activation(out=hnb, in_=x1[:, tb, :], func=AF.Square, accum_out=sm[:, 4:5]), [x1b[tb]], [aTb, smb])
                    rstd_from_ss(sm[:, 4:5], sm[:, 5:6], sm[:, 6:7], D, [smb], [smb])
                    S.op('dve', lambda e, tb=tb: e.scalar_tensor_tensor(out=x1[:, tb, :], in0=x1[:, tb, :], scalar=sm[:, 6:7], in1=fw, op0=ALU.mult, op1=ALU.mult),
                         [x1b[tb], smb, aTb], [x1b[tb]])
                    S.op('pool', lambda e, tb=tb: e.dma_start(out=yout[sg][t0 + tb * 128:t0 + (tb + 1) * 128, :], in_=x1[:, tb, :]), [x1b[tb]], [], key='y' + sg)
            chunks[-1]['post'] = fin
            run_chunks(chunks, slots, sbs, 256)

    phaseA('s')
    for k in ('AQK', 'AV', 'HQK', 'HV', 'HG', 'HD'):
        S.op('pool', lambda e, k=k: e.collective_compute("AllGather", ALU.bypass, replica_groups=RG, ins=[scr['s'][k]], outs=[gat[k]]),
             [sbufs['s'][k]], [gbufs[k]], key='ags' + k, inc=1)
    phaseA('p')
    phaseB_attn('p')
    phaseB_hgrn('p')
    S.region = 1
    phaseB_attn('s')
    phaseB_hgrn('s')
    S.region = None
    S.op('pool', lambda e: e.collective_compute("AllGather", ALU.bypass, replica_groups=RG, ins=[OTa_s], outs=[G_OTa]), [b_OTa_s], [b_GOTa], key='agoa', inc=1)
    S.op('pool', lambda e: e.collective_compute("AllGather", ALU.bypass, replica_groups=RG, ins=[OTh_s], outs=[G_OTh]), [b_OTh_s], [b_GOTh], key='agoh', inc=1)
    phaseC('p')
    S.region = 2
    phaseC('s')
    S.region = None
    S.simulate()
    S.emit(nc, ['yp', 'ys'], pid)
    return nc


def host_inputs(SEG, NF, x_prompt, x_sample, attn_norm_w, w_in, diff_lambda, subln_w, hgrn_lb, hgrn_norm_w,
                w_out, mlp_norm_w, w_up, w_down, final_norm_w):
    D = 4096
    f = np.float32
    xs = np.asarray(x_sample, f).reshape(8, SEG, D)
    xp = np.asarray(x_prompt, f)
    w_in = np.asarray(w_in, f)[0]; w_out = np.asarray(w_out, f)[0]; w_up = np.asarray(w_up, f)[0]; w_down = np.asarray(w_down, f)[0]
    inv = (1.0 / (np.float32(10000.0) ** (np.arange(0, 128, 2, dtype=f) / np.float32(128)))).astype(f)

    def tables(pos):
        ang = pos.astype(f)[:, None] * inv[None, :]
        ang = np.concatenate([ang, ang], axis=-1).astype(f)
        c = np.cos(ang).astype(f); s = np.sin(ang).astype(f)
        s[:, :64] = -s[:, :64]
        return np.ascontiguousarray(c), np.ascontiguousarray(s)
    ident = np.eye(128, dtype=f)
    tri = np.triu(np.ones((64, 64), f))
    masks = np.ascontiguousarray(np.concatenate([tri, tri.T], axis=1))
    scanm = np.ones((128, 512), f); scanm[:, ::64] = 0.0
    common = {
        "ancol": np.ascontiguousarray(np.asarray(attn_norm_w, f).reshape(32, 128).T),
        "mncol": np.ascontiguousarray(np.asarray(mlp_norm_w, f).reshape(32, 128).T),
        "finw": np.asarray(final_norm_w, f).reshape(1, D),
        "dlam": np.asarray(diff_lambda, f).reshape(1, 512),
        "subcol": np.ascontiguousarray(np.asarray(subln_w, f).reshape(2, 128).T),
        "lbt": np.ascontiguousarray(np.asarray(hgrn_lb, f).reshape(2, 2, 16, 128).transpose(3, 0, 1, 2).reshape(128, 64)),
        "hnw": np.asarray(hgrn_norm_w, f).reshape(1, 128),
        "ident": ident, "masks": masks, "scanm": scanm,
    }
    cp, sp_ = tables(np.arange(SEG))
    maps = []
    for c in range(8):
        cs, ss = tables(np.arange(c * SEG, (c + 1) * SEG))
        m = dict(common)
        m.update({
            "xp": np.ascontiguousarray(xp[c]), "xs": np.ascontiguousarray(xs[c]),
            "w_in_sh": np.ascontiguousarray(w_in[c * (D // 8):(c + 1) * (D // 8)]),
            "w_out_sh": np.ascontiguousarray(w_out[c * (D // 8):(c + 1) * (D // 8)]),
            "w_up_sh": np.ascontiguousarray(w_up[c * (D // 8):(c + 1) * (D // 8)]),
            "w_down_sh": np.ascontiguousarray(w_down[c * (NF // 8):(c + 1) * (NF // 8)]),
            "cosp": cp, "sinp": sp_, "coss": cs, "sins": ss,
        })
        maps.append(m)
    return maps


_NC_CACHE = {}


def run(SEG, NF, debug=False, **inputs):
    key = (SEG, NF, debug)
    if key not in _NC_CACHE:
        _NC_CACHE[key] = build(SEG, NF, debug)
    nc = _NC_CACHE[key]
    maps = host_inputs(SEG, NF, **inputs)
    res = run_bass_kernel_spmd(nc, maps, core_ids=list(range(8)))
    if debug:
        return res.results
    yp = np.stack([res.results[c]["yp"] for c in range(8)], axis=0)
    ys = np.concatenate([res.results[c]["ys"] for c in range(8)], axis=0)[None]
    return yp.astype(np.float32), ys.astype(np.float32)


def kernel(**inputs):
    return run(2048, 16384, **inputs)
```
